# Optimizing a Trainium2 kernel written in Bass

```python
import math
import jax, jax.numpy as jnp
from jax import lax
import numpy as np

D_MODEL = 1024
BATCH = 16
SEQ = 4096
DEPTH = 2
DEC_BATCH = 32
DEC_SEQ = 32
PAST_LEN = 2048

CHUNK = 64
D_CONV = 512
CONV_W = 31
H_RET = 8
DK_RET = 64
DV_RET = 64
D_RET = H_RET * DK_RET
H_FOX = 8
DH_FOX = 64
D_FOX = H_FOX * DH_FOX
D_FF = 2816
FFN_CONV_W = 3
Q_BLOCK = 128
ROPE_BASE = 10000.0
EPS = 1e-6
N_BRANCH = 3
SEG_SIZES = (D_CONV, D_CONV, D_RET, D_RET, H_RET * DV_RET, H_RET * DV_RET, D_FOX, D_FOX, D_FOX, H_FOX, N_BRANCH * D_MODEL)
D_IN = sum(SEG_SIZES)

kernel_name = 'hybrid_streaming_conv_retention_fox_step'


def split_in(proj):
    idx = []
    acc = 0
    for s in SEG_SIZES[:-1]:
        acc += s
        idx.append(acc)
    return jnp.split(proj, idx, axis=-1)


def rms_norm(x, g):
    xf = x.astype(jnp.float32)
    y = xf * lax.rsqrt(jnp.mean(xf * xf, axis=-1, keepdims=True) + EPS)
    return (y * g.astype(jnp.float32)).astype(x.dtype)


def layer_norm(x, g, b):
    xf = x.astype(jnp.float32)
    mu = jnp.mean(xf, axis=-1, keepdims=True)
    xc = xf - mu
    var = jnp.mean(xc * xc, axis=-1, keepdims=True)
    return (xc * lax.rsqrt(var + EPS) * g.astype(jnp.float32) + b.astype(jnp.float32)).astype(x.dtype)


def dwconv(xp, w):
    c = xp.shape[-1]
    return lax.conv_general_dilated(xp, w[:, None, :].astype(xp.dtype), window_strides=(1,), padding='VALID',
                                    dimension_numbers=('NWC', 'WIO', 'NWC'), feature_group_count=c)


def rotary(x, pos):
    half = x.shape[-1] // 2
    inv = jnp.exp(-math.log(ROPE_BASE) * jnp.arange(half, dtype=jnp.float32) / half)
    ang = pos.astype(jnp.float32)[:, None] * inv[None, :]
    cos = jnp.cos(ang)[None, :, None, :]
    sin = jnp.sin(ang)[None, :, None, :]
    xf = x.astype(jnp.float32)
    x1, x2 = xf[..., :half], xf[..., half:]
    return jnp.concatenate([x1 * cos - x2 * sin, x1 * sin + x2 * cos], axis=-1)


def retention_chunks(q, k, v, state0):
    c = q.shape[2]
    lg = jnp.log1p(-jnp.exp2(-5.0 - jnp.arange(H_RET, dtype=jnp.float32)))
    idx = jnp.arange(c, dtype=jnp.float32)
    intra = jnp.exp(lg[:, None, None] * jnp.abs(idx[:, None] - idx[None, :]))
    s = jnp.einsum('bnihd,bnjhd->bnhij', q, k) * intra
    o = jnp.einsum('bnhij,bnjhv->bnihv', s, v)
    k_end = k * jnp.exp(lg[None, :] * (c - 1.0 - idx)[:, None])[:, :, None]
    kv = jnp.einsum('bnjhd,bnjhv->nbhdv', k_end, v)
    decay_c = jnp.exp(lg * c)[None, :, None, None]

    def step(st, kv_n):
        return st * decay_c + kv_n, st

    s_final, s_prev = lax.scan(step, state0, kv)
    q_in = q * jnp.exp(lg[None, :] * (idx + 1.0)[:, None])[:, :, None]
    o = o + jnp.einsum('bnihd,nbhdv->bnihv', q_in, s_prev)
    return o, s_final


def fox_block(q, c_q, pos_q, k, v, c_k, pos_k):
    s = jnp.einsum('bthd,bshd->bhts', q, k).astype(jnp.float32) * (DH_FOX ** -0.5)
    bias = jnp.transpose(c_q, (0, 2, 1))[..., :, None] - jnp.transpose(c_k, (0, 2, 1))[..., None, :]
    mask = pos_k[None, :] <= pos_q[:, None]
    s = jnp.where(mask[None, None], s + bias, -jnp.inf)
    p = jax.nn.softmax(s, axis=-1)
    return jnp.einsum('bhts,bshd->bthd', p.astype(v.dtype), v)


def fox_prompt(q, k, v, c, pos):
    b, s, h, d = q.shape
    nb = s // Q_BLOCK
    qb = jnp.moveaxis(q.reshape(b, nb, Q_BLOCK, h, d), 1, 0)
    cb = jnp.moveaxis(c.reshape(b, nb, Q_BLOCK, h), 1, 0)
    pb = pos.reshape(nb, Q_BLOCK)
    ob = lax.map(lambda a: fox_block(a[0], a[1], a[2], k, v, c, pos), (qb, cb, pb))
    return jnp.moveaxis(ob, 0, 1).reshape(b, s, h, d)


def trunk_layer(x, pos, conv_buf, ret_state, fox_past, ffn_buf,
                g_mix, w_in, b_fgate, w_dw_conv, b_dw_conv, ln_conv_g, ln_conv_b, w_conv_out,
                gn_ret_g, w_ret_out, g_q_fox, g_k_fox, w_fox_out, w_out,
                g_ffn, w_ffn_up, w_ffn_dw, b_ffn_dw, w_ffn_down):
    b, t, _ = x.shape
    h = rms_norm(x, g_mix)
    glu_a, glu_b, rq, rk, rv, rg, fq, fk, fv, ff, gates = split_in(h @ w_in)

    u = glu_a * jax.nn.sigmoid(glu_b)
    up = jnp.concatenate([conv_buf.astype(u.dtype), u], axis=1)
    cv = dwconv(up, w_dw_conv) + b_dw_conv
    y_conv = jax.nn.silu(layer_norm(cv, ln_conv_g, ln_conv_b)) @ w_conv_out
    new_conv_buf = up[:, -(CONV_W - 1):]

    q = rotary(rq.reshape(b, t, H_RET, DK_RET), pos)
    k = rotary(rk.reshape(b, t, H_RET, DK_RET), pos) * (DK_RET ** -0.5)
    v = rv.reshape(b, t, H_RET, DV_RET).astype(jnp.float32)
    c = min(t, CHUNK)
    n = t // c
    o, new_ret = retention_chunks(q.reshape(b, n, c, H_RET, DK_RET), k.reshape(b, n, c, H_RET, DK_RET),
                                  v.reshape(b, n, c, H_RET, DV_RET), ret_state.astype(jnp.float32))
    o = o.reshape(b, t, H_RET, DV_RET)
    mu = jnp.mean(o, axis=-1, keepdims=True)
    oc = o - mu
    o = oc * lax.rsqrt(jnp.mean(oc * oc, axis=-1, keepdims=True) + EPS)
    o = (o.reshape(b, t, H_RET * DV_RET) * gn_ret_g.astype(jnp.float32)).astype(x.dtype)
    y_ret = (jax.nn.silu(rg) * o) @ w_ret_out

    fq = rms_norm(fq.reshape(b, t, H_FOX, DH_FOX), g_q_fox)
    fk = rms_norm(fk.reshape(b, t, H_FOX, DH_FOX), g_k_fox)
    fv = fv.reshape(b, t, H_FOX, DH_FOX)
    logf = jax.nn.log_sigmoid((ff + b_fgate).astype(jnp.float32))
    if fox_past is None:
        cum = jnp.cumsum(logf, axis=1)
        of = fox_prompt(fq, fk, fv, cum, pos)
    else:
        k_past, v_past, logf_past = fox_past
        p_len = k_past.shape[1]
        k_all = jnp.concatenate([k_past.astype(fk.dtype), fk], axis=1)
        v_all = jnp.concatenate([v_past.astype(fv.dtype), fv], axis=1)
        cum = jnp.cumsum(jnp.concatenate([logf_past.astype(jnp.float32), logf], axis=1), axis=1)
        pos_k = jnp.arange(p_len + t)
        of = fox_block(fq, cum[:, p_len:], pos, k_all, v_all, cum, pos_k)
    y_fox = of.reshape(b, t, D_FOX) @ w_fox_out

    g = jax.nn.sigmoid(gates.reshape(b, t, N_BRANCH, D_MODEL))
    merged = g[:, :, 0] * y_conv + g[:, :, 1] * y_ret + g[:, :, 2] * y_fox
    x = x + merged @ w_out

    h2 = rms_norm(x, g_ffn)
    a, gb = jnp.split(h2 @ w_ffn_up, 2, axis=-1)
    ap = jnp.concatenate([ffn_buf.astype(a.dtype), a], axis=1)
    a = dwconv(ap, w_ffn_dw) + b_ffn_dw
    x = x + (jax.nn.gelu(a, approximate=False) * gb) @ w_ffn_down
    new_ffn_buf = ap[:, -(FFN_CONV_W - 1):]

    return x, (new_conv_buf, new_ret.astype(x.dtype), fk, fv, logf.astype(x.dtype), new_ffn_buf)


def setup_inputs(seed: int = 0) -> dict:
    key = jax.random.key(seed)
    ks = jax.random.split(key, 32)
    f32 = jnp.float32

    def nrm(k, shape, scale):
        return jax.random.normal(k, shape, f32) * scale

    def gain(k, shape):
        return 1.0 + 0.05 * jax.random.normal(k, shape, f32)

    return {
        'x_prompt': nrm(ks[0], (BATCH, SEQ, D_MODEL), 1.0),
        'x_sample': nrm(ks[1], (DEC_BATCH, DEC_SEQ, D_MODEL), 1.0),
        'state_conv': nrm(ks[2], (DEPTH, DEC_BATCH, CONV_W - 1, D_CONV), 0.5),
        'state_ret': nrm(ks[3], (DEPTH, DEC_BATCH, H_RET, DK_RET, DV_RET), 0.5),
        'cache_fox_k': nrm(ks[4], (DEPTH, DEC_BATCH, PAST_LEN, H_FOX, DH_FOX), 1.0),
        'cache_fox_v': nrm(ks[5], (DEPTH, DEC_BATCH, PAST_LEN, H_FOX, DH_FOX), 1.0),
        'cache_fox_logf': jax.nn.log_sigmoid(3.0 + jax.random.normal(ks[6], (DEPTH, DEC_BATCH, PAST_LEN, H_FOX), f32)),
        'state_ffn_conv': nrm(ks[7], (DEPTH, DEC_BATCH, FFN_CONV_W - 1, D_FF), 0.5),
        'g_mix': gain(ks[8], (DEPTH, D_MODEL)),
        'w_in': nrm(ks[9], (DEPTH, D_MODEL, D_IN), D_MODEL ** -0.5),
        'b_fgate': jnp.linspace(1.0, 5.0, H_FOX, dtype=f32)[None, :] + nrm(ks[10], (DEPTH, H_FOX), 0.1),
        'w_dw_conv': nrm(ks[11], (DEPTH, CONV_W, D_CONV), CONV_W ** -0.5),
        'b_dw_conv': nrm(ks[12], (DEPTH, D_CONV), 0.02),
        'ln_conv_g': gain(ks[13], (DEPTH, D_CONV)),
        'ln_conv_b': nrm(ks[14], (DEPTH, D_CONV), 0.02),
        'w_conv_out': nrm(ks[15], (DEPTH, D_CONV, D_MODEL), D_CONV ** -0.5),
        'gn_ret_g': gain(ks[16], (DEPTH, H_RET * DV_RET)),
        'w_ret_out': nrm(ks[17], (DEPTH, H_RET * DV_RET, D_MODEL), (H_RET * DV_RET) ** -0.5),
        'g_q_fox': gain(ks[18], (DEPTH, DH_FOX)),
        'g_k_fox': gain(ks[19], (DEPTH, DH_FOX)),
        'w_fox_out': nrm(ks[20], (DEPTH, D_FOX, D_MODEL), D_FOX ** -0.5),
        'w_out': nrm(ks[21], (DEPTH, D_MODEL, D_MODEL), D_MODEL ** -0.5),
        'g_ffn': gain(ks[22], (DEPTH, D_MODEL)),
        'w_ffn_up': nrm(ks[23], (DEPTH, D_MODEL, 2 * D_FF), D_MODEL ** -0.5),
        'w_ffn_dw': nrm(ks[24], (DEPTH, FFN_CONV_W, D_FF), FFN_CONV_W ** -0.5),
        'b_ffn_dw': nrm(ks[25], (DEPTH, D_FF), 0.02),
        'w_ffn_down': nrm(ks[26], (DEPTH, D_FF, D_MODEL), D_FF ** -0.5),
    }


def reference(x_prompt, x_sample, state_conv, state_ret, cache_fox_k, cache_fox_v, cache_fox_logf, state_ffn_conv,
              g_mix, w_in, b_fgate, w_dw_conv, b_dw_conv, ln_conv_g, ln_conv_b, w_conv_out,
              gn_ret_g, w_ret_out, g_q_fox, g_k_fox, w_fox_out, w_out,
              g_ffn, w_ffn_up, w_ffn_dw, b_ffn_dw, w_ffn_down):
    bp, sp, _ = x_prompt.shape
    p_len = cache_fox_k.shape[2]
    ts = x_sample.shape[1]
    pos_p = jnp.arange(sp)
    pos_s = p_len + jnp.arange(ts)
    xp, xs = x_prompt, x_sample
    pc, pr, pk, pv, pf, pn = [], [], [], [], [], []
    sc, sr, sk, sv, sf, sn = [], [], [], [], [], []
    for l in range(DEPTH):
        lw = (g_mix[l], w_in[l], b_fgate[l], w_dw_conv[l], b_dw_conv[l], ln_conv_g[l], ln_conv_b[l], w_conv_out[l],
              gn_ret_g[l], w_ret_out[l], g_q_fox[l], g_k_fox[l], w_fox_out[l], w_out[l],
              g_ffn[l], w_ffn_up[l], w_ffn_dw[l], b_ffn_dw[l], w_ffn_down[l])
        xp, st_p = trunk_layer(xp, pos_p,
                               jnp.zeros((bp, CONV_W - 1, D_CONV), xp.dtype),
                               jnp.zeros((bp, H_RET, DK_RET, DV_RET), jnp.float32),
                               None,
                               jnp.zeros((bp, FFN_CONV_W - 1, D_FF), xp.dtype), *lw)
        xs, st_s = trunk_layer(xs, pos_s, state_conv[l], state_ret[l],
                               (cache_fox_k[l], cache_fox_v[l], cache_fox_logf[l]),
                               state_ffn_conv[l], *lw)
        pc.append(st_p[0]); pr.append(st_p[1]); pk.append(st_p[2]); pv.append(st_p[3]); pf.append(st_p[4]); pn.append(st_p[5])
        sc.append(st_s[0]); sr.append(st_s[1]); sk.append(st_s[2]); sv.append(st_s[3]); sf.append(st_s[4]); sn.append(st_s[5])
    return (xp, xs,
            jnp.stack(pc), jnp.stack(pr), jnp.stack(pk), jnp.stack(pv), jnp.stack(pf), jnp.stack(pn),
            jnp.stack(sc), jnp.stack(sr), jnp.stack(sk), jnp.stack(sv), jnp.stack(sf), jnp.stack(sn))
```

```python
import math
import numpy as np
from contextlib import ExitStack
import concourse.bass as bass
import concourse.mybir as mybir
from concourse.bass_utils import run_bass_kernel_spmd

F32 = mybir.dt.float32
BF16 = mybir.dt.bfloat16
ALU = mybir.AluOpType
AF = mybir.ActivationFunctionType
AX = mybir.AxisListType

D = 1024; KC = 8; DC = 512; NH = 8; HD = 64; DFF = 2816; NFC = 22; NIN = 7688; DEPTH = 2; EPS = 1e-6
CW = 31; NCORES = 8; SB_ = 4; LS = 32
COMPUTE = ("pe", "act", "dve", "pool")
DMAQ = ("sync", "pool", "act")


class Buf:
    __slots__ = ("name", "ap", "writer", "readers", "bf")

    def __init__(self, name, ap=None):
        self.name = name; self.ap = ap; self.writer = None; self.readers = []; self.bf = None

    def __getitem__(self, k):
        return self.ap[k]


def alias(new, olds):
    for o in olds:
        if o.writer is not None:
            new.readers.append(o.writer)
        new.readers.extend(o.readers)


class Prog:
    NSLOT = 24

    def __init__(self, nc):
        self.nc = nc
        self.streams = {e: [] for e in ("pe", "act", "dve", "pool", "sync")}
        self.cnt = {e: 0 for e in COMPUTE}
        self.slot_use = {q: [0] * self.NSLOT for q in DMAQ}
        self.slot_next = {q: 0 for q in DMAQ}
        self.waited = {e: {} for e in self.streams}
        self.out_tokens = []

    def _deps(self, eng, reads, writes):
        need = {}
        for b in reads:
            if b.writer is not None:
                sk, v = b.writer
                if need.get(sk, 0) < v: need[sk] = v
        for b in writes:
            if b.writer is not None:
                sk, v = b.writer
                if need.get(sk, 0) < v: need[sk] = v
            for (sk, v) in b.readers:
                if need.get(sk, 0) < v: need[sk] = v
        out = []
        w = self.waited[eng]
        for sk, v in need.items():
            if sk == ("c", "pe") and eng == "pe":
                continue
            if w.get(sk, 0) >= v:
                continue
            w[sk] = v
            out.append((sk, v))
        return out

    def _commit(self, tok, reads, writes):
        for b in reads:
            b.readers.append(tok)
            if len(b.readers) > 48:
                m = {}
                for (sk, v) in b.readers:
                    if m.get(sk, 0) < v: m[sk] = v
                b.readers = list(m.items())
        for b in writes:
            b.writer = tok; b.readers = []

    def op(self, eng, fn, reads=(), writes=()):
        waits = self._deps(eng, reads, writes)
        self.cnt[eng] += 1
        tok = (("c", eng), self.cnt[eng])
        self.streams[eng].append((fn, waits, ("c", eng), 1))
        self._commit(tok, reads, writes)
        return tok

    def dma(self, q, fn, reads=(), writes=(), is_out=False):
        s = self.slot_next[q]
        self.slot_next[q] = (s + 1) % self.NSLOT
        sk = ("d", q, s)
        waits = self._deps(q, reads, writes)
        prev = self.slot_use[q][s]
        if prev > 0 and self.waited[q].get(sk, 0) < 16 * prev:
            self.waited[q][sk] = 16 * prev
            waits.append((sk, 16 * prev))
        self.slot_use[q][s] = prev + 1
        tok = (sk, 16 * (prev + 1))
        self.streams[q].append((fn, waits, sk, 16))
        self._commit(tok, reads, writes)
        if is_out:
            self.out_tokens.append(tok)
        return tok

    def sem_keys(self):
        ks = [("c", e) for e in COMPUTE]
        for q in DMAQ:
            ks += [("d", q, s) for s in range(self.NSLOT)]
        return ks

    def emit_all(self, es):
        nc = self.nc
        sems = {k: es.enter_context(nc.semaphore("s_" + "_".join(map(str, k)))) for k in self.sem_keys()}
        m = {}
        for (sk, v) in self.out_tokens:
            if m.get(sk, 0) < v: m[sk] = v
        final_waits = [(sk, v) for sk, v in m.items() if self.waited["sync"].get(sk, 0) < v]
        streams = self.streams

        def mk(en):
            def f(eng):
                for (fn, waits, sk, inc) in streams[en]:
                    for (wk, v) in waits:
                        eng.wait_ge(sems[wk], v)
                    fn(eng).then_inc(sems[sk], inc)
                if en == "sync":
                    for (wk, v) in final_waits:
                        eng.wait_ge(sems[wk], v)
            return f
        with nc.Block() as block:
            for en, deco in [("pe", block.tensor), ("act", block.scalar), ("dve", block.vector),
                             ("pool", block.gpsimd), ("sync", block.sync)]:
                deco(mk(en))


def const_tables(SEQ, PAST):
    f = np.float32
    c = {}
    c["c_ident"] = np.eye(128, dtype=f)
    k = np.arange(128)
    c["c_ones"] = np.ones((128, 128), f)
    c["c_tri_p"] = (k[:, None] <= k[None, :]).astype(f)
    c["c_tri_s"] = ((k[:, None] <= k[None, :]) & (k[:, None] // LS == k[None, :] // LS)).astype(f)
    partner = np.where((k % 64) < 32, k + 32, k - 32)
    perm = np.zeros((128, 128), f); perm[partner, k] = 1.0
    c["c_perm"] = perm
    c["c_causal"] = np.where(k[:, None] <= k[None, :], 0.0, -30000.0).astype(f)
    c["c_msamp"] = np.where((k[:, None] <= k[None, :]) & (k[:, None] // LS == k[None, :] // LS), 0.0, -30000.0).astype(f)
    lg = np.log1p(-np.exp2(-5.0 - np.arange(NH, dtype=np.float64)))
    j = k[:, None, None]; i = k[None, None, :]; lgh = lg[None, :, None]
    same = (j // 64 == i // 64)
    Mp = np.where(same, np.exp(lgh * np.abs(i - j)), np.where(j // 64 < i // 64, np.exp(lgh * (i - j)), 0.0)) / 8.0
    c["c_Mp"] = Mp.astype(f)
    Ms = np.where(j // LS == i // LS, np.exp(lgh * np.abs(i - j)), 0.0) / 8.0
    c["c_Ms"] = Ms.astype(f)
    p = k[:, None, None]; cc = np.arange(4)[None, :, None]; ii = k[None, None, :]
    hh = 2 * cc + p // 64
    c["c_qdec_p"] = np.exp(lg[hh] * (ii + 1.0)).astype(f)
    c["c_qdec_s"] = np.exp(lg[hh] * ((ii % LS) + 1.0)).astype(f)
    c["c_kdec_p"] = (np.exp(lg[None, :] * (127.0 - k[:, None])) / 8.0).astype(f)
    ks = np.zeros((SB_, 128, NH), f)
    for s in range(SB_):
        ks[s] = np.where((k[:, None] // LS) == s, np.exp(lg[None, :] * (LS - 1.0 - (k[:, None] % LS))) / 8.0, 0.0)
    c["c_kdec_s"] = ks
    h2 = 2 * np.arange(4)[None, :] + (k[:, None] // 64)
    c["c_sdec_p"] = np.exp(lg[h2] * 128.0).astype(f)
    c["c_sdec_s"] = np.exp(lg[h2] * float(LS)).astype(f)
    half = 32
    inv = np.exp(-math.log(10000.0) * np.arange(half, dtype=f) / half).astype(f)

    def rot(pos):
        ang = (pos.astype(f)[None, :] * inv[k % 32][:, None]).astype(f)
        cs = np.cos(ang).astype(f); sn = np.sin(ang).astype(f)
        sn = np.where(((k % 64) < 32)[:, None], -sn, sn).astype(f)
        return cs, sn
    c["c_cos_p"], c["c_sin_p"] = rot(np.arange(SEQ))
    c["c_cos_s"], c["c_sin_s"] = rot(np.tile(PAST + np.arange(LS), SB_))
    sm = np.zeros((128, SB_), f)
    for s in range(SB_):
        sm[s * LS:(s + 1) * LS, s] = 1.0
    c["c_segmask"] = sm
    return c


W_SHAPES = {
    "g_mix": (DEPTH, D), "w_in": (DEPTH, D, NIN), "b_fgate": (DEPTH, NH), "w_dw_conv": (DEPTH, CW, DC),
    "b_dw_conv": (DEPTH, DC), "ln_conv_g": (DEPTH, DC), "ln_conv_b": (DEPTH, DC), "w_conv_out": (DEPTH, DC, D),
    "gn_ret_g": (DEPTH, 512), "w_ret_out": (DEPTH, 512, D), "g_q_fox": (DEPTH, HD), "g_k_fox": (DEPTH, HD),
    "w_fox_out": (DEPTH, 512, D), "w_out": (DEPTH, D, D), "g_ffn": (DEPTH, D), "w_ffn_up": (DEPTH, D, 2 * DFF),
    "w_ffn_dw": (DEPTH, 3, DFF), "b_ffn_dw": (DEPTH, DFF), "w_ffn_down": (DEPTH, DFF, D),
}


class Kern:
    def __init__(self, NSP, SEQ, PAST):
        self.NSP, self.SEQ, self.PAST = NSP, SEQ, PAST
        self.nc = bass.Bass("TRN2", target_bir_lowering=False)
        self.es = ExitStack()

    def din(self, name, shape):
        return self.nc.dram_tensor(name, list(shape), F32, kind="ExternalInput").ap()

    def dout(self, name, shape):
        return self.nc.dram_tensor(name, list(shape), F32, kind="ExternalOutput").ap()

    def dscr(self, name, shape, dt=BF16):
        return self.nc.dram_tensor(name, list(shape), dt).ap()

    def sb(self, name, shape, dt=F32):
        return Buf(name, self.es.enter_context(self.nc.sbuf_tensor(name, list(shape), dt)))

    def psb(self, name, shape, dt=F32):
        return Buf(name, self.es.enter_context(self.nc.psum_tensor(name, list(shape), dt)))

    def mm(self, out, lhsT, rhs, R, W, start=True, stop=True):
        self.P.op("pe", lambda e: e.matmul(out, lhsT=lhsT, rhs=rhs, start=start, stop=stop), R, W)

    def tr(self, out, in_, ident, R, W):
        self.P.op("pe", lambda e: e.transpose(out=out, in_=in_, identity=ident), R, W)

    def act(self, out, in_, func, R, W, bias=None, scale=None, accum=None):
        kw = {}
        if bias is not None: kw["bias"] = bias
        if scale is not None: kw["scale"] = scale
        if accum is not None: kw["accum_out"] = accum
        self.P.op("act", lambda e: e.activation(out=out, in_=in_, func=func, **kw), R, W)

    def tt(self, eng, out, in0, in1, op, R, W):
        self.P.op(eng, lambda e: e.tensor_tensor(out=out, in0=in0, in1=in1, op=op), R, W)

    def ts(self, eng, out, in0, s1, op0, R, W, s2=None, op1=None):
        if op1 is None:
            self.P.op(eng, lambda e: e.tensor_scalar(out=out, in0=in0, scalar1=s1, scalar2=None, op0=op0), R, W)
        else:
            self.P.op(eng, lambda e: e.tensor_scalar(out=out, in0=in0, scalar1=s1, scalar2=s2, op0=op0, op1=op1), R, W)

    def stt(self, eng, out, in0, scalar, in1, op0, op1, R, W):
        self.P.op(eng, lambda e: e.scalar_tensor_tensor(out=out, in0=in0, scalar=scalar, in1=in1, op0=op0, op1=op1), R, W)

    def cp(self, eng, out, in_, R, W):
        if eng == "act":
            self.P.op("act", lambda e: e.copy(out=out, in_=in_), R, W)
        else:
            self.P.op(eng, lambda e: e.tensor_copy(out=out, in_=in_), R, W)

    def red(self, out, in_, R, W):
        self.P.op("dve", lambda e: e.tensor_reduce(out=out, in_=in_, axis=AX.X, op=ALU.add), R, W)

    def recip(self, out, in_, R, W):
        self.P.op("dve", lambda e: e.reciprocal(out=out, in_=in_), R, W)

    def memset(self, eng, ap, val, W):
        self.P.op(eng, lambda e: e.memset(ap, val), (), W)

    def dma(self, q, out, in_, R, W, is_out=False, slow=False):
        if slow:
            self.P.dma(q, lambda e: e.dma_start(out=out, in_=in_, allow_slow_non_contiguous=True), R, W, is_out)
        else:
            self.P.dma(q, lambda e: e.dma_start(out=out, in_=in_), R, W, is_out)

    def bank(self):
        while True:
            b = self.banks[self.bank_i % len(self.banks)]
            self.bank_i += 1
            if b not in self.held:
                return b

    def hold(self, n):
        got = []
        for b in self.banks[::-1]:
            if b not in self.held and len(got) < n:
                got.append(b)
        for b in got:
            self.held.add(b)
        return got

    def release(self, bs):
        for b in bs:
            self.held.discard(b)

    def tbank(self):
        return self.bank()

    def tmp32(self):
        b = self.t32[self.t32_i % len(self.t32)]; self.t32_i += 1
        return b

    def tmp16(self):
        b = self.t16[self.t16_i % len(self.t16)]; self.t16_i += 1
        return b

    def sm(self):
        b = self.smalls[self.sm_i % len(self.smalls)]; self.sm_i += 1
        return b

    def prefetch(self, key, npart=128):
        b = self.slab_load(key, npart)
        self.prefetched[key] = b

    def slab_load(self, key, npart=128):
        if key in self.prefetched:
            return self.prefetched.pop(key)
        dram_ap, dbuf, kc, ncol = self.slabs[key]
        sbuf = self.slab_bufs[self.slab_i % len(self.slab_bufs)]; self.slab_i += 1
        self.dma("sync", sbuf[0:npart, 0:kc, 0:ncol], dram_ap[:, 0:kc, 0:ncol], [dbuf], [sbuf])
        return sbuf

    def build(self):
        nc = self.nc; NSP, SEQ, PAST = self.NSP, self.SEQ, self.PAST
        self.P = Prog(nc)
        NT = SEQ // 512
        I = {}
        I["xp"] = self.din("xp", (NSP, SEQ, D)); I["xs"] = self.din("xs", (128, D))
        I["st_conv"] = self.din("st_conv", (DEPTH, SB_ * 30, DC)); I["st_ret"] = self.din("st_ret", (DEPTH, SB_, NH, HD, HD))
        I["ck"] = self.din("ck", (DEPTH, SB_, PAST, 512)); I["cv"] = self.din("cv", (DEPTH, SB_, PAST, 512))
        I["clf"] = self.din("clf", (DEPTH, SB_, PAST, NH)); I["st_ffn"] = self.din("st_ffn", (DEPTH, SB_ * 2, DFF))
        for n, s in W_SHAPES.items():
            I[n] = self.din(n, s)
        ct = const_tables(SEQ, PAST)
        for n, v in ct.items():
            I[n] = self.din(n, v.shape)
        self.I = I
        O = {}
        O["y_p"] = self.dout("y_p", (NSP, SEQ, D)); O["y_s"] = self.dout("y_s", (128, D))
        O["p_conv"] = self.dout("p_conv", (DEPTH, NSP, 30, DC)); O["p_ret"] = self.dout("p_ret", (DEPTH, NSP, NH, HD, HD))
        O["p_k"] = self.dout("p_k", (DEPTH, NSP, SEQ, 512)); O["p_v"] = self.dout("p_v", (DEPTH, NSP, SEQ, 512))
        O["p_lf"] = self.dout("p_lf", (DEPTH, NSP, SEQ, NH)); O["p_ffn"] = self.dout("p_ffn", (DEPTH, NSP, 2, DFF))
        O["s_conv"] = self.dout("s_conv", (DEPTH, SB_ * 30, DC)); O["s_ret"] = self.dout("s_ret", (DEPTH, SB_, NH, HD, HD))
        O["s_k"] = self.dout("s_k", (DEPTH, 128, 512)); O["s_v"] = self.dout("s_v", (DEPTH, 128, 512))
        O["s_lf"] = self.dout("s_lf", (DEPTH, 128, NH)); O["s_ffn"] = self.dout("s_ffn", (DEPTH, SB_ * 2, DFF))
        self.O = O

        self.slabs = {}
        self.prefetched = {}

        self.kts_p = [[self.dscr(f"ktp{l}_{s}", (65, NH, SEQ)) for s in range(NSP)] for l in range(DEPTH)]
        self.vs_p = [[self.dscr(f"vsp{l}_{s}", (SEQ, NH, HD)) for s in range(NSP)] for l in range(DEPTH)]
        self.kts_s = [[self.dscr(f"kts{l}_{s}", (65, NH, PAST)) for s in range(SB_)] for l in range(DEPTH)]
        self.vs_s = [[self.dscr(f"vss{l}_{s}", (PAST, NH, HD)) for s in range(SB_)] for l in range(DEPTH)]
        self.kvb_p = {}; self.kvb_s = {}

        sb = self.sb
        self.x = sb("x", (128, 4, D))
        self.xb = [Buf(f"xb{i}", self.x.ap[:, i, :]) for i in range(4)]
        self.hT = sb("hT", (128, KC, 512), BF16)
        self.slab_bufs = [sb(f"slabbuf{i}", (128, 8, 512), BF16) for i in range(4)]; self.slab_i = 0
        self.ofT = sb("ofT", (64, NH, 512), BF16)
        self.zrT = sb("zrT", (128, 4, 512), BF16)
        self.zcT = sb("zcT", (128, 4, 512), BF16)
        self.t32 = [sb(f"t32_{i}", (128, 512)) for i in range(4)]; self.t32_i = 0
        self.t16 = [sb(f"t16_{i}", (128, 512), BF16) for i in range(3)]; self.t16_i = 0
        self.smalls = [sb(f"sm{i}", (128, 16)) for i in range(12)]; self.sm_i = 0
        self.arena32 = self.es.enter_context(nc.sbuf_tensor("arena32", [128, 4096], F32))
        self.arena16 = self.es.enter_context(nc.sbuf_tensor("arena16", [128, 15488], BF16))
        self.ident = sb("ident", (128, 128)); self.ident_b = sb("ident_b", (128, 128), BF16)
        self.ones = sb("ones", (128, 128)); self.tri = sb("tri", (128, 128))
        self.perm_b = sb("perm_b", (128, 128), BF16); self.cmask = sb("cmask", (128, 128), BF16)
        self.Mtab = sb("Mtab", (128, NH, 128)); self.qdec = sb("qdec", (128, 4, 128))
        self.kdec = sb("kdec", (128, 4, NH)); self.sdec = sb("sdec", (128, 4))
        self.cosT = sb("cosT", (128, 512)); self.sinT = sb("sinT", (128, 512))
        self.segmask = sb("segmask", (128, SB_)); self.epsb = sb("epsb", (128, 1))
        self.gmixT = sb("gmixT", (128, DEPTH, KC)); self.gffnT = sb("gffnT", (128, DEPTH, KC))
        self.wconvT = sb("wconvT", (128, DEPTH, 4, CW)); self.bconvT = sb("bconvT", (128, DEPTH, 4))
        self.lngT = sb("lngT", (128, DEPTH, 4)); self.lnbT = sb("lnbT", (128, DEPTH, 4))
        self.wffnT = sb("wffnT", (128, DEPTH, NFC, 3)); self.bffnT = sb("bffnT", (128, DEPTH, NFC))
        self.gnrep = sb("gnrep", (128, DEPTH, 512)); self.gqrep = sb("gqrep", (128, DEPTH, HD)); self.gkrep = sb("gkrep", (128, DEPTH, HD))
        self.bfrep = sb("bfrep", (128, DEPTH, NH))
        self.wff = sb("wff", (128, DEPTH, KC, NH), BF16)
        self.u_ctx = [sb(f"u_ctx{l}", (128, 4, SB_, 30)) for l in range(DEPTH)]
        self.a_ctx = [sb(f"a_ctx{l}", (128, NFC, SB_, 2)) for l in range(DEPTH)]
        self.rstate = [sb(f"rstate{l}", (128, SB_, 4, HD)) for l in range(DEPTH)]
        self.rstate_b = [sb(f"rstate_b{l}", (128, SB_, 4, HD), BF16) for l in range(DEPTH)]
        self.nch_p = [sb(f"nch_p{l}", (128, SEQ // 128, NH)) for l in range(DEPTH)]
        self.nch_s = [[sb(f"nch_s{l}_{s}", (128, PAST // 128, NH)) for s in range(SB_)] for l in range(DEPTH)]
        self.ncur_s = [sb(f"ncur_s{l}", (128, 1, NH)) for l in range(DEPTH)]
        self.carry = [sb(f"carry{l}", (128, NH)) for l in range(DEPTH)]
        self.banks = [self.psb(f"bank{i}", (128, 512)) for i in range(8)]; self.bank_i = 0; self.held = set()
        for b_ in self.banks:
            b_.bf = b_.ap[:, :].bitcast(BF16)

        import os as _os
        self.stage = int(_os.environ.get("KSTAGE", "9"))
        self.setup_consts()
        self.weight_prologue()
        self.w_pump(10 ** 6)
        self.common_alloc()
        self.prompt_group(NT)
        self.sample_group()
        self.P.emit_all(self.es)
        self.es.close()
        return nc

    def setup_consts(self):
        I = self.I
        def load(buf, dst_ap, src_ap, slow=False):
            self.dma("act" if slow else "sync", dst_ap, src_ap, [], [buf], slow=slow)
        load(self.ident, self.ident[:], I["c_ident"][:, :]); load(self.ones, self.ones[:], I["c_ones"][:, :])
        load(self.segmask, self.segmask[:], I["c_segmask"][:, :])
        self.cp("dve", self.ident_b[:], self.ident[:], [self.ident], [self.ident_b])
        t = self.tmp32()
        load(t, t[:, 0:128], I["c_perm"][:, :])
        self.cp("dve", self.perm_b[:], t[:, 0:128], [t], [self.perm_b])
        self.memset("dve", self.epsb[:], EPS, [self.epsb])
        for l in range(DEPTH):
            load(self.gmixT, self.gmixT[:, l, :], I["g_mix"][l].rearrange("(c p) -> p c", p=128), slow=True)
            load(self.gffnT, self.gffnT[:, l, :], I["g_ffn"][l].rearrange("(c p) -> p c", p=128), slow=True)
            for c in range(4):
                load(self.wconvT, self.wconvT[:, l, c, :], I["w_dw_conv"][l][:, c * 128:(c + 1) * 128].rearrange("k p -> p k"), slow=True)
            load(self.bconvT, self.bconvT[:, l, :], I["b_dw_conv"][l].rearrange("(c p) -> p c", p=128), slow=True)
            load(self.lngT, self.lngT[:, l, :], I["ln_conv_g"][l].rearrange("(c p) -> p c", p=128), slow=True)
            load(self.lnbT, self.lnbT[:, l, :], I["ln_conv_b"][l].rearrange("(c p) -> p c", p=128), slow=True)
            for k in range(3):
                load(self.wffnT, self.wffnT[:, l, :, k], I["w_ffn_dw"][l][k].rearrange("(c p) -> p c", p=128), slow=True)
            load(self.bffnT, self.bffnT[:, l, :], I["b_ffn_dw"][l].rearrange("(c p) -> p c", p=128), slow=True)
            load(self.gnrep, self.gnrep[:, l, :], I["gn_ret_g"][l:l + 1, :].partition_broadcast(128))
            load(self.gqrep, self.gqrep[:, l, :], I["g_q_fox"][l:l + 1, :].partition_broadcast(128))
            load(self.gkrep, self.gkrep[:, l, :], I["g_k_fox"][l:l + 1, :].partition_broadcast(128))
            load(self.bfrep, self.bfrep[:, l, :], I["b_fgate"][l:l + 1, :].partition_broadcast(128))
            self.dma("pool", self.wff[:, l, :, :], I["w_in"][l][:, 4608:4616].rearrange("(kc p) n -> p kc n", p=128), [], [self.wff])

    def load_group_tables(self, grp):
        I = self.I
        def load(buf, dst_ap, src_ap):
            self.dma("sync", dst_ap, src_ap, [], [buf])
        sfx = "_" + grp
        load(self.tri, self.tri[:], I["c_tri" + sfx][:, :])
        load(self.Mtab, self.Mtab[:], I["c_M" + grp][:, :, :])
        t = self.tmp32()
        load(t, t[:, 0:128], I["c_causal" if grp == "p" else "c_msamp"][:, :])
        self.cp("dve", self.cmask[:], t[:, 0:128], [t], [self.cmask])
        load(self.sdec, self.sdec[:], I["c_sdec" + sfx][:, :])
        if grp == "p":
            load(self.qdec, self.qdec[:, :, :], I["c_qdec_p"][:, :, :])
            load(self.kdec, self.kdec[:, 0, :], I["c_kdec_p"][:, :])
        else:
            load(self.qdec, self.qdec[:, :, :], I["c_qdec_s"][:, :, :])
            for s in range(SB_):
                load(self.kdec, self.kdec[:, s, :], I["c_kdec_s"][s])
            load(self.cosT, self.cosT[:, 0:128], I["c_cos_s"][:, :]); load(self.sinT, self.sinT[:, 0:128], I["c_sin_s"][:, :])

    def w_pump(self, n):
        while n > 0 and self.wq:
            self.wq.pop(0)()
            n -= 1

    def weight_prologue(self):
        I = self.I
        self.wq = []

        def cvt(key, src_ap, npart=128, kc=8, ncol=512):
            ap = self.dscr("w_" + "_".join(map(str, key)), (npart, 8, 512))
            b = Buf("slab" + str(key))
            self.slabs[key] = (ap, b, kc, ncol)
            self.wq.append(lambda: self.dma("pool", ap[:, 0:kc, 0:ncol], src_ap, [], [b]))
        for l in range(DEPTH):
            win = I["w_in"][l]
            for s in (0, 1, 6, 7, 8, 2, 3, 4, 5):
                cvt(("in", l, s), win[:, s * 512:(s + 1) * 512].rearrange("(kc p) n -> p kc n", p=128))
            def gate(g):
                c0 = 4616 + g * 512
                cvt(("gate", l, g), win[:, c0:c0 + 512].rearrange("(kc p) n -> p kc n", p=128))
            for hf in range(2):
                cs = slice(hf * 512, (hf + 1) * 512)
                cvt(("fo", l, hf), I["w_fox_out"][l][:, cs].rearrange("(h d) n -> d h n", d=64), npart=64)
                gate(4 + hf)
            for hf in range(2):
                cs = slice(hf * 512, (hf + 1) * 512)
                cvt(("ro", l, hf), I["w_ret_out"][l][:, cs].rearrange("(kc p) n -> p kc n", p=128), kc=4)
                gate(2 + hf)
            for hf in range(2):
                cs = slice(hf * 512, (hf + 1) * 512)
                cvt(("co", l, hf), I["w_conv_out"][l][:, cs].rearrange("(kc p) n -> p kc n", p=128), kc=4)
                gate(hf)
            for hf in range(2):
                cs = slice(hf * 512, (hf + 1) * 512)
                cvt(("wo", l, hf), I["w_out"][l][:, cs].rearrange("(kc p) n -> p kc n", p=128))
            wup = I["w_ffn_up"][l]
            for s in range(6):
                nco = 512 if s < 5 else 256
                cvt(("ua", l, s), wup[:, s * 512:s * 512 + nco].rearrange("(kc p) n -> p kc n", p=128), ncol=nco)
                cvt(("ub", l, s), wup[:, DFF + s * 512:DFF + s * 512 + nco].rearrange("(kc p) n -> p kc n", p=128), ncol=nco)
            wdn = I["w_ffn_down"][l]
            for hf in range(2):
                for g in range(3):
                    nk = 8 if g < 2 else 6
                    cvt(("dn", l, g, hf), wdn[g * 1024:g * 1024 + nk * 128, hf * 512:(hf + 1) * 512].rearrange("(kc p) n -> p kc n", p=128), kc=nk)

    def rmsnorm_T(self, G, gT, l):
        nb = G["nblk"]; T = G["T"]
        for b in range(nb):
            xb = self.xb[b]
            ss = self.sm()
            self.memset("pool", ss[:, :], 0.0, [ss])
            for hf in range(2):
                j = self.tmp32()
                self.act(j[:, :], xb[:, hf * 512:(hf + 1) * 512], AF.Square, [xb], [j, ss], accum=ss[:, hf:hf + 1])
            self.tt("dve", ss[:, 2:3], ss[:, 0:1], ss[:, 1:2], ALU.add, [ss], [ss])
            self.act(ss[:, 2:3], ss[:, 2:3], AF.Sqrt, [ss, self.epsb], [ss], bias=self.epsb[:, 0:1], scale=1.0 / D)
            self.recip(ss[:, 2:3], ss[:, 2:3], [ss], [ss])
            for hf in range(2):
                t = self.tmp16()
                if hf == 0:
                    self.ts("dve", t[:, :], xb[:, hf * 512:(hf + 1) * 512], ss[:, 2:3], ALU.mult, [xb, ss], [t])
                else:
                    self.act(t[:, :], xb[:, hf * 512:(hf + 1) * 512], AF.Copy, [xb, ss], [t], scale=ss[:, 2:3])
                tb = self.tbank()
                for c in range(4):
                    self.tr(tb.bf[:, c * 128:(c + 1) * 128], t[:, c * 128:(c + 1) * 128], self.ident_b[:], [t, self.ident_b], [tb])
                self.tt("dve", self.hT[:, hf * 4:(hf + 1) * 4, b * 128:(b + 1) * 128], tb.bf[:, 0:512].rearrange("p (c t) -> p c t", t=128),
                        gT[:, l, hf * 4:(hf + 1) * 4].unsqueeze(2).to_broadcast([128, 4, 128]), ALU.mult, [tb, gT], [self.hT])

    def proj_fm(self, slab, sub, rhsT, kc_n, T, R):
        ps = self.bank()
        for kc in range(kc_n):
            self.mm(ps[:, 0:T], slab[:, kc, sub * 128:(sub + 1) * 128], rhsT[:, kc, 0:T], [slab] + R, [ps], start=(kc == 0), stop=(kc == kc_n - 1))
        return ps

    def proj_tm(self, slab, b, ncol=512):
        ps = self.bank()
        for kc in range(KC):
            self.mm(ps[:, 0:ncol], self.hT[:, kc, b * 128:(b + 1) * 128], slab[:, kc, 0:ncol], [slab, self.hT], [ps], start=(kc == 0), stop=(kc == KC - 1))
        return ps

    def fox_phase(self, G, l, tinfo):
        I, O = self.I, self.O
        T, nb, nseg, L = G["T"], G["nblk"], G["nseg"], G["L"]
        grp = G["name"]
        a16 = self.arena16; a32 = self.arena32
        o = [0]

        def c16(name, npart, n, shape=None):
            ap = a16[0:npart, o[0]:o[0] + n]; o[0] += n
            return Buf(name, ap)
        qaug = [c16(f"qaug{i}", 128, NH * 66) for i in range(2)]
        kaug = self.kaug
        qT = c16("qT_aug", 65, NH * 512); kTc = c16("kT_cur", 65, NH * 512)
        vcur = self.vcur
        hk = [c16(f"hk{i}", 65, 4 * 512) for i in range(2)]
        PT = [c16(f"PT{i}", 128, 512) for i in range(4)]
        knf = [Buf(f"knf{i}", a32[:, i * 512:(i + 1) * 512]) for i in range(2)]
        vf = [Buf(f"vf{i}", a32[:, 1024 + i * 512:1024 + (i + 1) * 512]) for i in range(2)]
        newb = qaug + [qT, kTc] + hk + PT + knf + vf
        for b_ in newb:
            alias(b_, self.arena_users + self.arena_sticky)
        self.arena_users = newb + [self.vcur] + self.kaug + self.hv
        self.arena_sticky = []
        s_in = {k: self.slab_load(("in", l, k)) for k in (6, 7, 8)}
        nch_cur = []
        lfs = {}

        def stage1(b):
            qa = qaug[b % 2]; ka = kaug[b % 2]; kn = knf[b % 2]; v32 = vf[b % 2]
            fps = self.bank()
            for kc in range(KC):
                self.mm(fps[:, 0:NH], self.hT[:, kc, b * 128:(b + 1) * 128], self.wff[:, l, kc, :], [self.hT, self.wff], [fps], start=(kc == 0), stop=(kc == KC - 1))
            fz = self.sm(); lf = self.sm()
            self.tt("dve", fz[:, 0:NH], fps[:, 0:NH], self.bfrep[:, l, :], ALU.add, [fps, self.bfrep], [fz])
            self.act(fz[:, 0:NH], fz[:, 0:NH], AF.Exp, [fz], [fz], scale=-1.0)
            self.act(fz[:, 0:NH], fz[:, 0:NH], AF.Ln, [fz], [fz], bias=1.0, scale=1.0)
            self.ts("dve", lf[:, 0:NH], fz[:, 0:NH], -1.0, ALU.mult, [fz], [lf])
            qps = self.proj_tm(s_in[6], b); kps = self.proj_tm(s_in[7], b); vps = self.proj_tm(s_in[8], b)
            cps = self.bank()
            self.mm(cps[:, 0:NH], self.tri[:], lf[:, 0:NH], [self.tri, lf], [cps])
            self.mm(cps[:, 16:16 + NH], self.ones[:], lf[:, 0:NH], [self.ones, lf], [cps])
            if grp == "p":
                nbuf = self.nch_p[l]; nap = nbuf[:, tinfo["t0"] // 128 + b, :]
            else:
                nbuf = self.ncur_s[l]; nap = nbuf[:, 0, :]
            nch_cur.append((nbuf, nap))
            self.stt("dve", nap, cps[:, 0:NH], -1.0, self.carry[l][:, :], ALU.mult, ALU.subtract, [cps, self.carry[l]], [nbuf])
            self.ts("dve", qa[:, :].rearrange("p (h d) -> p h d", d=66)[:, :, HD:HD + 1], nap.unsqueeze(2), -8.0, ALU.mult, [nbuf], [qa])
            if grp == "p":
                self.tt("dve", self.carry[l][:, :], self.carry[l][:, :], cps[:, 16:16 + NH], ALU.add, [cps, self.carry[l]], [self.carry[l]])
            sq_q = self.tmp32(); sq_k = self.tmp32(); ssq = self.sm()
            self.act(sq_k[:, :], kps[:, :], AF.Square, [kps], [sq_k])
            self.act(sq_q[:, :], qps[:, :], AF.Square, [qps], [sq_q])
            self.red(ssq[:, NH:2 * NH], sq_k[:, :].rearrange("p (h d) -> p h d", d=HD), [sq_k], [ssq])
            self.red(ssq[:, 0:NH], sq_q[:, :].rearrange("p (h d) -> p h d", d=HD), [sq_q], [ssq])
            self.act(ssq[:, 0:2 * NH], ssq[:, 0:2 * NH], AF.Sqrt, [ssq, self.epsb], [ssq], bias=self.epsb[:, 0:1], scale=1.0 / HD)
            self.recip(ssq[:, 0:2 * NH], ssq[:, 0:2 * NH], [ssq], [ssq])
            self.tt("dve", sq_k[:, :].rearrange("p (h d) -> p h d", d=HD), kps[:, :].rearrange("p (h d) -> p h d", d=HD),
                    ssq[:, NH:2 * NH].unsqueeze(2).to_broadcast([128, NH, HD]), ALU.mult, [kps, ssq], [sq_k])
            self.tt("dve", sq_q[:, :].rearrange("p (h d) -> p h d", d=HD), qps[:, :].rearrange("p (h d) -> p h d", d=HD),
                    ssq[:, 0:NH].unsqueeze(2).to_broadcast([128, NH, HD]), ALU.mult, [qps, ssq], [sq_q])
            self.tt("pool", kn[:, :].rearrange("p (h d) -> p h d", d=HD), sq_k[:, :].rearrange("p (h d) -> p h d", d=HD),
                    self.gkrep[:, l, :].unsqueeze(1).to_broadcast([128, NH, HD]), ALU.mult, [sq_k, self.gkrep], [kn])
            self.tt("pool", qa[:, :].rearrange("p (h d) -> p h d", d=66)[:, :, 0:HD], sq_q[:, :].rearrange("p (h d) -> p h d", d=HD),
                    self.gqrep[:, l, :].unsqueeze(1).to_broadcast([128, NH, HD]), ALU.mult, [sq_q, self.gqrep], [qa])
            self.cp("act", ka[:, :, 0:HD], kn[:, :].rearrange("p (h d) -> p h d", d=HD), [kn], [ka])
            self.cp("act", v32[:, :], vps[:, :], [vps], [v32])
            if grp == "p":
                sq_, t0 = tinfo["seq"], tinfo["t0"] + b * 128
                self.dma("pool", O["p_k"][l, sq_, t0:t0 + 128, :], kn[:, :], [kn], [], is_out=True)
                self.dma("pool", O["p_v"][l, sq_, t0:t0 + 128, :], v32[:, :], [v32], [], is_out=True)
                self.dma("pool", O["p_lf"][l, sq_, t0:t0 + 128, :], lf[:, 0:NH], [lf], [], is_out=True)
            else:
                self.dma("pool", O["s_k"][l, :, :], kn[:, :], [kn], [], is_out=True)
                self.dma("pool", O["s_v"][l, :, :], v32[:, :], [v32], [], is_out=True)
                self.dma("pool", O["s_lf"][l, :, :], lf[:, 0:NH], [lf], [], is_out=True)
            self.cp("act", vcur[:, b, :, 0:HD], v32[:, :].rearrange("p (h d) -> p h d", d=HD), [v32], [vcur])
            self.bg_pump(2)

        def stage2(b):
            qa = qaug[b % 2]; ka = kaug[b % 2]
            for (src, srcv, dst) in ((ka, ka[:, :, :], kTc), (qa, qa[:, :].rearrange("p (h d) -> p h d", d=66), qT)):
                for hg in range(2):
                    tb = self.tbank()
                    for hh in range(4):
                        self.tr(tb.bf[0:65, hh * 128:(hh + 1) * 128], srcv[:, hg * 4 + hh, 0:65], self.ident_b[:], [src, self.ident_b], [tb])
                    self.cp("dve" if hg == 0 else "act", dst[0:65, :].rearrange("p (h t) -> p h t", t=512)[:, hg * 4:(hg + 1) * 4, b * 128:(b + 1) * 128],
                            tb.bf[0:65, 0:512].rearrange("p (h t) -> p h t", t=128), [tb], [dst])
        for b in range(nb + 1):
            if b < nb:
                stage1(b)
            if b >= 1:
                stage2(b - 1)
        if grp == "p" and not tinfo["last"]:
            sq_, t0 = tinfo["seq"], tinfo["t0"]
            kb = Buf("kvb"); self.kvb_p[(l, sq_, t0 // 512)] = kb
            self.dma("pool", self.kts_p[l][sq_][:, :, t0:t0 + 512], kTc[0:65, :].rearrange("p (h t) -> p h t", t=512), [kTc], [kb])
            for b in range(4):
                self.dma("pool", self.vs_p[l][sq_][t0 + b * 128:t0 + (b + 1) * 128, :, :], vcur[:, b, :, 0:HD], [vcur], [kb])
        self.prefetch(("in", l, 2)); self.prefetch(("in", l, 3))
        qTv = qT[0:65, :].rearrange("p (h t) -> p h t", t=512)
        kTv = kTc[0:65, :].rearrange("p (h t) -> p h t", t=512)
        LOOK = 3
        for si in range(nseg):
            q0 = si * L
            if grp == "p":
                nhist = tinfo["t0"] // 512; hist = [(self.kts_p[l][tinfo["seq"]], self.vs_p[l][tinfo["seq"]], self.kvb_p[(l, tinfo["seq"], c)], self.nch_p[l], c) for c in range(nhist)]
            else:
                sq_ = si
                nhist = self.PAST // 512; hist = [(self.kts_s[l][sq_], self.vs_s[l][sq_], self.kvb_s[(l, sq_, c)], self.nch_s[l][sq_], c) for c in range(nhist)]
            for hg in range(2):
                obs = self.hold(4)
                started = [False] * 4
                items = []
                for ci in range(nhist + 1):
                    if ci < nhist:
                        kd, vd, kb, nbuf, c = hist[ci]
                        hkb = hk[self.hk_i % 2]; hvb = self.hv[self.hk_i % 2]; self.hk_i += 1

                        def loader(kd=kd, vd=vd, kb=kb, c=c, hkb=hkb, hvb=hvb):
                            self.dma("sync", hkb[0:65, :].rearrange("p (h t) -> p h t", t=512), kd[:, hg * 4:(hg + 1) * 4, c * 512:(c + 1) * 512], [kb], [hkb])
                            for b in range(4):
                                self.dma("sync", hvb[:, b, :, 0:HD], vd[c * 512 + b * 128:c * 512 + (b + 1) * 128, hg * 4:(hg + 1) * 4, :], [kb], [hvb])
                        kview = hkb[0:65, :].rearrange("p (h t) -> p h t", t=512); vview = hvb; kR = hkb; vR = hvb
                        nkb = 4; hoff = 0
                    else:
                        loader = None
                        kview = kTv; vview = vcur; kR = kTc; vR = vcur; nkb = nb; hoff = hg * 4
                    for hh in range(4):
                        h = hg * 4 + hh
                        for j in range(nkb):
                            if ci < nhist:
                                qa_, qb_ = 0, L; mask = None; mn = 0
                                bias = nbuf[:, c * 4 + j, h:h + 1]; bR = nbuf
                            else:
                                bR, nap = nch_cur[j]
                                bias = nap[:, h:h + 1]
                                if grp == "p":
                                    qa_, qb_ = 128 * j, L; mask = self.cmask[:, 0:128]; mn = 128
                                else:
                                    qa_, qb_ = 0, L; mask = self.cmask[:, si * LS:(si + 1) * LS]; mn = LS
                            items.append(dict(pre=(loader if (hh == 0 and j == 0) else None), pump=(len(items) % 2 == 0),
                                              klhs=kview[:, hoff + hh, j * 128:(j + 1) * 128], kR=kR, vlhs=vview[:, j, hoff + hh, :], vR=vR,
                                              bias=bias, bR=bR, qa=qa_, qb=qb_, mask=mask, mn=mn, hh=hh, h=h,
                                              last=(ci == nhist and j == nkb - 1)))

                def front(it):
                    if it["pre"] is not None:
                        it["pre"]()
                    if it["pump"]:
                        self.bg_pump(1)
                    n = it["qb"] - it["qa"]
                    S = self.bank()
                    self.mm(S[:, 0:n], it["klhs"], qTv[:, it["h"], q0 + it["qa"]:q0 + it["qb"]], [it["kR"], qT], [S], start=True, stop=(it["mask"] is None))
                    if it["mask"] is not None:
                        self.mm(S[:, 0:it["mn"]], self.ident_b[:], it["mask"], [self.ident_b, self.cmask], [S], start=False, stop=True)
                    pt = PT[self.pt_i % 4]; self.pt_i += 1
                    self.act(pt[:, 0:n], S[:, 0:n], AF.Exp, [S, it["bR"]], [pt], bias=it["bias"], scale=0.125)
                    it["pt"] = pt

                def back(it):
                    n = it["qb"] - it["qa"]; hh = it["hh"]; pt = it["pt"]
                    self.mm(obs[hh][:, it["qa"]:it["qb"]], it["vlhs"], pt[:, 0:n], [it["vR"], pt], [obs[hh]], start=(not started[hh]), stop=it["last"])
                    started[hh] = True
                for i in range(len(items) + LOOK):
                    if i < len(items):
                        front(items[i])
                    if i >= LOOK:
                        back(items[i - LOOK])
                for hh in range(4):
                    h = hg * 4 + hh
                    rd = self.tmp32()
                    self.act(rd[0:64, 0:L], obs[hh][64:128, 0:L], AF.Ln, [obs[hh]], [rd])
                    self.act(rd[0:64, 0:L], rd[0:64, 0:L], AF.Exp, [rd], [rd], scale=-1.0)
                    self.tt("dve", self.ofT[0:64, h, q0:q0 + L], obs[hh][0:64, 0:L], rd[0:64, 0:L], ALU.mult, [obs[hh], rd], [self.ofT])
                self.release(obs)

    def bg_pump(self, n, eng="dve"):
        while n > 0 and self.bg:
            self.bg.pop(0)(eng)
            n -= 1

    def ret_phase(self, G, l, tinfo):
        T, nb, nseg, L = G["T"], G["nblk"], G["nseg"], G["L"]
        grp = G["name"]
        a16 = self.arena16; a32 = self.arena32
        o = [0]

        def c16(name, n):
            ap = a16[:, o[0]:o[0] + n]; o[0] += n
            return Buf(name, ap)
        qT = c16("rqT", 4 * 512); kT = c16("rkT", 4 * 512); vb = c16("rvb", 4 * 512)
        kend = [c16(f"kend{i}", 512) for i in range(4)]; qin = [c16(f"qin{i}", 512) for i in range(4)]
        stm = [c16(f"stm{i}", 512) for i in range(4)]
        rgs = Buf("rgs", a32[:, 0:2048])
        osbs = [Buf(f"osb{i}", a32[:, 2048 + i * 512:2048 + (i + 1) * 512]) for i in range(4)]
        newb = [qT, kT, vb] + kend + qin + stm + [rgs] + osbs
        for b_ in newb:
            alias(b_, self.arena_users + self.arena_sticky)
        self.arena_users = newb
        s_q = self.slab_load(("in", l, 2)); s_k = self.slab_load(("in", l, 3))
        ritems = [(slab, dst, sub) for (slab, dst) in ((s_q, qT), (s_k, kT)) for sub in range(4)]

        def rfront(it):
            slab, dst, sub = it
            ps = self.proj_fm(slab, sub, self.hT, KC, T, [self.hT])
            raw = self.rraw[self.rraw_i % 3]; self.rraw_i += 1
            self.cp("act", raw[:, 0:T], ps[:, 0:T], [ps], [raw])
            return raw

        def rback(it, raw):
            slab, dst, sub = it
            pp = self.bank()
            self.mm(pp[:, 0:T], self.perm_b[:], raw[:, 0:T], [self.perm_b, raw], [pp])
            t1 = self.tmp32(); t2 = self.tmp32()
            self.tt("pool", t1[:, 0:T], raw[:, 0:T], self.cosT[:, 0:T], ALU.mult, [raw, self.cosT], [t1])
            self.tt("dve", t2[:, 0:T], pp[:, 0:T], self.sinT[:, 0:T], ALU.mult, [pp, self.sinT], [t2])
            self.tt("dve", dst[:, sub * 512:sub * 512 + T], t1[:, 0:T], t2[:, 0:T], ALU.add, [t1, t2], [dst])
        raws = {}
        for i in range(len(ritems) + 1):
            if i < len(ritems):
                raws[i] = rfront(ritems[i])
            if i >= 1:
                rback(ritems[i - 1], raws[i - 1])
        s_v = self.slab_load(("in", l, 4)); s_g = self.slab_load(("in", l, 5))
        for b in range(nb):
            ps = self.proj_tm(s_v, b)
            self.cp("act", vb[:, b * 512:(b + 1) * 512], ps[:, :], [ps], [vb])
            ps = self.proj_tm(s_g, b)
            self.act(rgs[:, b * 512:(b + 1) * 512], ps[:, :], AF.Silu, [ps], [rgs])
            self.tt("pool", rgs[:, b * 512:(b + 1) * 512], rgs[:, b * 512:(b + 1) * 512], self.gnrep[:, l, :], ALU.mult, [rgs, self.gnrep], [rgs])
        qTv = qT[:, :].rearrange("p (c t) -> p c t", t=512); kTv = kT[:, :].rearrange("p (c t) -> p c t", t=512)
        vbv = vb[:, :].rearrange("p (b h d) -> p b h d", h=NH, d=HD)
        st = self.rstate[l]; stb = self.rstate_b[l]
        nsub = 1 if grp == "p" else SB_
        def pa_front(b):
            bs = slice(b * 128, (b + 1) * 128); bp = b % 2
            tb = self.tbank()
            for c in range(4):
                self.tr(tb.bf[:, c * 128:(c + 1) * 128], kTv[:, c, bs], self.ident_b[:], [kT, self.ident_b], [tb])
            for s in range(nsub):
                ks = kend[s if grp == "s" else bp]; qs = qin[s if grp == "s" else bp]
                self.tt("dve", ks[:, :].rearrange("p (h d) -> p h d", d=HD), tb.bf[:, 0:512].rearrange("p (h d) -> p h d", d=HD),
                        self.kdec[:, s, :].unsqueeze(2).to_broadcast([128, NH, HD]), ALU.mult, [tb, self.kdec], [ks])
                if grp == "p":
                    self.tt("pool", qs[:, :].rearrange("p (c t) -> p c t", t=128), qTv[:, :, bs], self.qdec[:, :, :], ALU.mult, [qT, self.qdec], [qs])
                else:
                    cs = slice(s * LS, (s + 1) * LS)
                    self.memset("pool", qs[:, :], 0.0, [qs])
                    self.tt("dve", qs[:, :].rearrange("p (c t) -> p c t", t=128)[:, :, cs], qTv[:, :, b * 128 + s * LS:b * 128 + (s + 1) * LS], self.qdec[:, :, cs], ALU.mult, [qT, self.qdec], [qs])
            for par in range(2):
                sps = self.bank()
                pr = slice(par * 64, par * 64 + 64)
                for c in range(4):
                    self.mm(sps[:, c * 128:(c + 1) * 128], kTv[pr, c, bs], qTv[pr, c, bs], [kT, qT], [sps])
                sm_ = stm[par + 2 * bp]
                self.tt("dve", sm_[:, 0:512].rearrange("p (h t) -> p h t", t=128), sps[:, :].rearrange("p (h t) -> p h t", t=128),
                        self.Mtab[:, :, :].rearrange("p (c two) t -> p c two t", two=2)[:, :, par, :], ALU.mult, [sps, self.Mtab], [sm_])

        def pa_back(b):
            bp = b % 2
            ops = self.bank()
            for h in range(NH):
                pr = slice((h % 2) * 64, (h % 2) * 64 + 64)
                sm_ = stm[(h % 2) + 2 * bp]
                self.mm(ops[:, h * HD:(h + 1) * HD], sm_[:, (h // 2) * 128:(h // 2 + 1) * 128], vbv[:, b, h, :], [sm_, vb], [ops], start=True, stop=False)
                for s in range(nsub):
                    seg = s if grp == "s" else 0
                    qs = qin[s if grp == "s" else bp]
                    self.mm(ops[:, h * HD:(h + 1) * HD], qs[pr, (h // 2) * 128:(h // 2 + 1) * 128], stb[pr, seg, h // 2, :], [qs, stb], [ops], start=False, stop=(s == nsub - 1))
            osb = osbs[b]
            self.cp("act", osb[:, :], ops[:, :], [ops], [osb])
            for s in range(nsub):
                seg = s if grp == "s" else 0
                ks = kend[s if grp == "s" else bp]
                kv = self.bank()
                for h in range(NH):
                    self.mm(kv[:, h * HD:(h + 1) * HD], ks[:, (h // 2) * 128:(h // 2 + 1) * 128], vbv[:, b, h, :], [ks, vb], [kv])
                kvv = kv[:, :].rearrange("p (c two d) -> p c two d", two=2, d=HD)
                self.tt("dve", st[:, seg, :, :], st[:, seg, :, :], self.sdec[:, :].unsqueeze(2).to_broadcast([128, 4, HD]), ALU.mult, [st, self.sdec], [st])
                self.tt("dve", st[0:64, seg, :, :], st[0:64, seg, :, :], kvv[0:64, :, 0, :], ALU.add, [st, kv], [st])
                self.tt("dve", st[64:128, seg, :, :], st[64:128, seg, :, :], kvv[64:128, :, 1, :], ALU.add, [st, kv], [st])
                self.cp("act", stb[:, seg, :, :], st[:, seg, :, :], [st], [stb])
        for b in range(nb + 1):
            if b < nb:
                pa_front(b)
            if b >= 1:
                pa_back(b - 1)
        ogs = [c16(f"og{i}", 512) for i in range(4)]
        for b_ in ogs:
            alias(b_, self.arena_users + self.arena_sticky)
        self.arena_users = self.arena_users + ogs
        self.arena_sticky = ogs
        for b in range(nb):
            osb = osbs[b]
            sq = self.tmp32(); s1 = self.sm(); s2 = self.sm()
            self.red(s1[:, 0:NH], osb[:, :].rearrange("p (h d) -> p h d", d=HD), [osb], [s1])
            self.act(sq[:, :], osb[:, :], AF.Square, [osb], [sq])
            self.red(s2[:, 0:NH], sq[:, :].rearrange("p (h d) -> p h d", d=HD), [sq], [s2])
            self.ts("dve", s1[:, 0:NH], s1[:, 0:NH], 1.0 / HD, ALU.mult, [s1], [s1])
            self.tt("dve", s1[:, 8:16], s1[:, 0:NH], s1[:, 0:NH], ALU.mult, [s1], [s1])
            self.stt("dve", s2[:, 0:NH], s2[:, 0:NH], 1.0 / HD, s1[:, 8:16], ALU.mult, ALU.subtract, [s1, s2], [s2])
            self.act(s2[:, 0:NH], s2[:, 0:NH], AF.Sqrt, [s2, self.epsb], [s2], bias=self.epsb[:, 0:1], scale=1.0)
            self.recip(s2[:, 0:NH], s2[:, 0:NH], [s2], [s2])
            o3 = osb[:, :].rearrange("p (h d) -> p h d", d=HD)
            self.tt("pool", o3, o3, s1[:, 0:NH].unsqueeze(2).to_broadcast([128, NH, HD]), ALU.subtract, [osb, s1], [osb])
            self.tt("dve", o3, o3, s2[:, 0:NH].unsqueeze(2).to_broadcast([128, NH, HD]), ALU.mult, [osb, s2], [osb])
            self.tt("dve", ogs[b][:, :], osb[:, :], rgs[:, b * 512:(b + 1) * 512], ALU.mult, [osb, rgs], [ogs[b]])

        def ret_finish():
            for b in range(nb):
                bs = slice(b * 128, (b + 1) * 128)
                tb2 = self.tbank()
                for c in range(4):
                    self.tr(tb2.bf[:, c * 128:(c + 1) * 128], ogs[b][:, c * 128:(c + 1) * 128], self.ident_b[:], [ogs[b], self.ident_b], [tb2])
                self.cp("act", self.zrT[:, :, bs], tb2.bf[:, 0:512].rearrange("p (c t) -> p c t", t=128), [tb2], [self.zrT])
        self.ret_finish = ret_finish

    def conv_front(self, G, l, tinfo):
        T, nb, nseg, L = G["T"], G["nblk"], G["nseg"], G["L"]
        grp = G["name"]; O = self.O
        W = 30 + L
        uext = self.uext; cv = self.cv
        uv = uext[:, 0:4 * nseg * W].rearrange("p (c s w) -> p c s w", c=4, s=nseg)
        cvv = cv[:, 0:4 * T].rearrange("p (c s t) -> p c s t", c=4, s=nseg)
        uc = self.u_ctx[l]
        self.cp("pool", uv[:, :, :, 0:30], uc[:, :, 0:nseg, :], [uc], [uext])
        s_a = self.slab_load(("in", l, 0)); s_b = self.slab_load(("in", l, 1))
        for sub in range(4):
            pa = self.proj_fm(s_a, sub, self.hT, KC, T, [self.hT]); pb = self.proj_fm(s_b, sub, self.hT, KC, T, [self.hT])
            sg = self.tmp32()
            self.act(sg[:, 0:T], pb[:, 0:T], AF.Sigmoid, [pb], [sg])
            self.tt("dve", uv[:, sub, :, 30:30 + L], pa[:, 0:T].rearrange("p (s t) -> p s t", s=nseg), sg[:, 0:T].rearrange("p (s t) -> p s t", s=nseg), ALU.mult, [pa, sg], [uext])
        self.cp("pool", uc[:, :, 0:nseg, :], uv[:, :, :, L:L + 30], [uext], [uc])
        if tinfo["last"]:
            pst = self.bank(); n30 = nseg * 30
            for c in range(4):
                cp_ = self.tmp32()
                self.cp("pool", cp_[:, 0:n30].rearrange("p (s w) -> p s w", s=nseg), uv[:, c, :, L:L + 30], [uext], [cp_])
                self.tr(pst[0:n30, c * 128:(c + 1) * 128], cp_[:, 0:n30], self.ident[:], [cp_, self.ident], [pst])
            ot = self.tmp32()
            self.cp("act", ot[0:n30, :], pst[0:n30, :], [pst], [ot])
            if grp == "p":
                self.dma("pool", O["p_conv"][l, tinfo["seq"], :, :], ot[0:30, :], [ot], [], is_out=True)
            else:
                self.dma("pool", O["s_conv"][l, :, :], ot[0:n30, :], [ot], [], is_out=True)
        def tap0(c):
            return lambda eng: self.ts(eng, cvv[:, c, :, :], uv[:, c, :, 0:L], self.wconvT[:, l, c, 0:1], ALU.mult, [uext, self.wconvT, self.bconvT], [cv], s2=self.bconvT[:, l, c:c + 1], op1=ALU.add)

        def tapk(c, k):
            return lambda eng: self.stt(eng, cvv[:, c, :, :], uv[:, c, :, k:k + L], self.wconvT[:, l, c, k:k + 1], cvv[:, c, :, :], ALU.mult, ALU.add, [uext, self.wconvT, cv], [cv])
        for c in range(4):
            self.bg.append(tap0(c))
            for k in range(1, CW):
                self.bg.append(tapk(c, k))

    def conv_tail(self, G, l, tinfo):
        T = G["T"]
        self.bg_pump(10 ** 6, eng="dve")
        cv = self.cv
        cf = cv[:, 0:4 * T].rearrange("p (c t) -> p c t", c=4)
        s1 = self.bank(); s2 = self.bank()
        for c in range(4):
            self.mm(s1[:, 0:T], self.ones[:], cf[:, c, :], [self.ones, cv], [s1], start=(c == 0), stop=(c == 3))
        for c in range(4):
            sq = self.tmp32()
            self.act(sq[:, 0:T], cf[:, c, :], AF.Square, [cv], [sq])
            self.mm(s2[:, 0:T], self.ones[:], sq[:, 0:T], [self.ones, sq], [s2], start=(c == 0), stop=(c == 3))
        mean = self.tmp32(); var = self.tmp32()
        self.ts("dve", mean[:, 0:T], s1[:, 0:T], 1.0 / DC, ALU.mult, [s1], [mean])
        self.tt("dve", var[:, 0:T], mean[:, 0:T], mean[:, 0:T], ALU.mult, [mean], [var])
        self.stt("dve", var[:, 0:T], s2[:, 0:T], 1.0 / DC, var[:, 0:T], ALU.mult, ALU.subtract, [s2, var], [var])
        self.act(var[:, 0:T], var[:, 0:T], AF.Ln, [var, self.epsb], [var], bias=self.epsb[:, 0:1], scale=1.0)
        self.act(var[:, 0:T], var[:, 0:T], AF.Exp, [var], [var], scale=-0.5)
        for c in range(4):
            self.tt("pool", cf[:, c, :], cf[:, c, :], mean[:, 0:T], ALU.subtract, [cv, mean], [cv])
            self.tt("dve", cf[:, c, :], cf[:, c, :], var[:, 0:T], ALU.mult, [cv, var], [cv])
            self.act(self.zcT[:, c, 0:T], cf[:, c, :], AF.Silu, [cv, self.lngT, self.lnbT], [self.zcT], bias=self.lnbT[:, l, c:c + 1], scale=self.lngT[:, l, c:c + 1])

    def merge_phase(self, G, l, tinfo):
        T, nb = G["T"], G["nblk"]
        a32 = self.arena32; a16 = self.arena16
        mg = Buf("merged", a32[:, 0:4096]); mgb = Buf("merged_b", a16[:, 0:4096])
        for b_ in (mg, mgb):
            alias(b_, self.arena_users + self.arena_sticky)
        self.arena_users = [mg, mgb]
        mgv = mg[:, :].rearrange("p (c t) -> p c t", t=512); mbv = mgb[:, :].rearrange("p (c t) -> p c t", t=512)
        branches = (("fo", self.ofT, 8, 64, 2), ("ro", self.zrT, 4, 128, 1), ("co", self.zcT, 4, 128, 0))
        for bi, (wk, zT, nk, kp, gi) in enumerate(branches):
            if bi == 1 and self.ret_finish is not None:
                self.ret_finish(); self.ret_finish = None
            for hf in range(2):
                if bi == 1 and hf == 1:
                    self.conv_tail(G, l, tinfo)
                sw = self.slab_load((wk, l, hf), npart=kp); sg_ = self.slab_load(("gate", l, 2 * gi + hf))
                for sub in range(4):
                    cc = hf * 4 + sub
                    gp = self.proj_fm(sg_, sub, self.hT, KC, T, [self.hT])
                    yp = self.bank()
                    for kc in range(nk):
                        self.mm(yp[:, 0:T], sw[0:kp, kc, sub * 128:(sub + 1) * 128], zT[0:kp, kc, 0:T], [sw, zT], [yp], start=(kc == 0), stop=(kc == nk - 1))
                    sg = self.tmp32()
                    self.act(sg[:, 0:T], gp[:, 0:T], AF.Sigmoid, [gp], [sg])
                    if bi == 0:
                        self.tt("dve", mgv[:, cc, 0:T], sg[:, 0:T], yp[:, 0:T], ALU.mult, [sg, yp], [mg])
                    else:
                        t2 = self.tmp32()
                        self.tt("dve", t2[:, 0:T], sg[:, 0:T], yp[:, 0:T], ALU.mult, [sg, yp], [t2])
                        if bi == 1:
                            self.tt("pool", mgv[:, cc, 0:T], mgv[:, cc, 0:T], t2[:, 0:T], ALU.add, [mg, t2], [mg])
                        else:
                            self.tt("pool", mbv[:, cc, 0:T], mgv[:, cc, 0:T], t2[:, 0:T], ALU.add, [mg, t2], [mgb])
                    if bi < 2:
                        self.bg_pump(4)
        sws = [self.slab_load(("wo", l, hf)) for hf in range(2)]
        for b in range(nb):
            for hf in range(2):
                sw = sws[hf]
                ps = self.bank()
                for kc in range(KC):
                    self.mm(ps[:, :], mbv[:, kc, b * 128:(b + 1) * 128], sw[:, kc, :], [mgb, sw], [ps], start=(kc == 0), stop=(kc == KC - 1))
                xs = self.xb[b][:, hf * 512:(hf + 1) * 512]
                self.tt("dve", xs, xs, ps[:, :], ALU.add, [self.xb[b], ps], [self.xb[b]])

    def ffn_phase(self, G, l, tinfo):
        T, nb, nseg, L = G["T"], G["nblk"], G["nseg"], G["L"]
        grp = G["name"]; O = self.O
        a32 = self.arena32; a16 = self.arena16
        actT = Buf("actT", a16[:, 0:NFC * 512])
        W2 = 2 + L
        aext = [Buf(f"aext{i}", a32[:, i * 600:i * 600 + nseg * W2]) for i in range(3)]
        for b_ in [actT] + aext:
            alias(b_, self.arena_users + self.arena_sticky)
        self.arena_users = [actT] + aext
        av = actT[:, :].rearrange("p (c t) -> p c t", t=512)
        ac = self.a_ctx[l]
        n2 = 2 * nseg
        lastv = self.hT[:, :, 0:T].rearrange("p k (s t) -> p k s t", s=nseg)[:, :, :, L - 2:L]
        for s in range(6):
            sa = self.slab_load(("ua", l, s)); sbb = self.slab_load(("ub", l, s))
            nsub = 4 if s < 5 else 2
            if tinfo["last"]:
                pl = self.bank(); nco = nsub * 128
                for kc in range(KC):
                    lt = self.sm16()
                    self.cp("pool", lt[:, 0:n2].rearrange("p (s t) -> p s t", s=nseg), lastv[:, kc, :, :], [self.hT], [lt])
                    self.mm(pl[0:n2, 0:nco], lt[:, 0:n2], sa[:, kc, 0:nco], [lt, sa], [pl], start=(kc == 0), stop=(kc == KC - 1))
                ob = self.tmp32()
                self.cp("act", ob[0:n2, 0:nco], pl[0:n2, 0:nco], [pl], [ob])
                if grp == "p":
                    self.dma("pool", O["p_ffn"][l, tinfo["seq"], :, s * 512:s * 512 + nco], ob[0:2, 0:nco], [ob], [], is_out=True)
                else:
                    self.dma("pool", O["s_ffn"][l, :, s * 512:s * 512 + nco], ob[0:n2, 0:nco], [ob], [], is_out=True)
            for sub in range(nsub):
                i = s * 4 + sub
                pa = self.proj_fm(sa, sub, self.hT, KC, T, [self.hT]); pb = self.proj_fm(sbb, sub, self.hT, KC, T, [self.hT])
                ae = aext[i % 3]
                aev = ae[:, :].rearrange("p (s w) -> p s w", s=nseg)
                pav = pa[:, 0:T].rearrange("p (s t) -> p s t", s=nseg)
                self.cp("pool", aev[:, :, 0:2], ac[:, i, 0:nseg, :], [ac], [ae])
                self.cp("act", aev[:, :, 2:2 + L], pav, [pa], [ae])
                self.cp("pool", ac[:, i, 0:nseg, :], aev[:, :, L:L + 2], [ae], [ac])
                t1 = self.tmp32()
                t1v = t1[:, 0:T].rearrange("p (s t) -> p s t", s=nseg)
                self.act(t1[:, 0:T], pa[:, 0:T], AF.Identity, [pa, self.wffnT, self.bffnT], [t1], bias=self.bffnT[:, l, i:i + 1], scale=self.wffnT[:, l, i, 2:3])
                self.stt("dve", t1v, aev[:, :, 0:L], self.wffnT[:, l, i, 0:1], t1v, ALU.mult, ALU.add, [ae, self.wffnT, t1], [t1])
                self.stt("dve", t1v, aev[:, :, 1:1 + L], self.wffnT[:, l, i, 1:2], t1v, ALU.mult, ALU.add, [ae, self.wffnT, t1], [t1])
                self.act(t1[:, 0:T], t1[:, 0:T], AF.Gelu, [t1], [t1])
                self.tt("dve", av[:, i, 0:T], t1[:, 0:T], pb[:, 0:T], ALU.mult, [t1, pb], [actT])
        for hf in range(2):
            sl = [self.slab_load(("dn", l, g, hf)) for g in range(3)]
            for b in range(nb):
                ps = self.bank()
                for i in range(NFC):
                    self.mm(ps[:, :], av[:, i, b * 128:(b + 1) * 128], sl[i // 8][:, i % 8, :], [actT, sl[i // 8]], [ps], start=(i == 0), stop=(i == NFC - 1))
                xs = self.xb[b][:, hf * 512:(hf + 1) * 512]
                self.tt("dve", xs, xs, ps[:, :], ALU.add, [self.xb[b], ps], [self.xb[b]])

    def sm16(self):
        b = self.sm16s[self.sm16_i % len(self.sm16s)]; self.sm16_i += 1
        return b

    def layer(self, G, l, tinfo):
        import os as _os
        ksub = int(_os.environ.get("KSUB", "9"))
        if l > 0 and ksub < 9:
            return
        self.rmsnorm_T(G, self.gmixT, l)
        self.conv_front(G, l, tinfo)
        if ksub >= 2: self.fox_phase(G, l, tinfo)
        if ksub >= 3: self.ret_phase(G, l, tinfo)
        if ksub >= 5: self.merge_phase(G, l, tinfo)
        if ksub >= 6: self.rmsnorm_T(G, self.gffnT, l)
        if ksub >= 7: self.ffn_phase(G, l, tinfo)

    def common_alloc(self):
        sb = self.sb
        self.kaug = [sb(f"kaug{i}", (128, NH, 66), BF16) for i in range(2)]
        self.vcur = sb("vcur", (128, 4, NH, 128), BF16)
        self.hv = [sb(f"hv{i}", (128, 4, 4, 128), BF16) for i in range(2)]
        self.sm16s = [sb(f"sm16_{i}", (128, 16), BF16) for i in range(4)]; self.sm16_i = 0
        self.rraw = self.t16; self.rraw_i = 0
        self.hk_i = 0; self.pt_i = 0
        self.arena_users = []
        self.bg = []
        self.arena_sticky = []
        self.ret_finish = None
        convA = self.es.enter_context(self.nc.sbuf_tensor("convA", [128, 4224], F32))
        self.uext = Buf("uext", convA[:, 0:2176]); self.cv = Buf("cv", convA[:, 2176:4224])
        for k_ in self.kaug:
            self.memset("pool", k_[:, :, :], 1.0, [k_])
        self.memset("pool", self.vcur[:, :, :, :], 1.0, [self.vcur])
        for h_ in self.hv:
            self.memset("pool", h_[:, :, :, :], 1.0, [h_])

    def sample_group(self):
        I, O = self.I, self.O
        PAST = self.PAST
        G = {"name": "s", "T": 128, "nblk": 1, "nseg": SB_, "L": LS}
        npb = PAST // 128
        a32 = self.arena32
        kTts = [Buf(f"kTt{i}", self.arena16[0:65, i * 4096:(i + 1) * 4096]) for i in range(2)]
        ksts = [Buf(f"kst{i}", self.arena16[:, 8192 + i * 2048:8192 + (i + 1) * 2048]) for i in range(2)]
        self.kst_i = 0
        for k_ in kTts:
            self.memset("pool", k_[64:65, :], 1.0, [k_])
        lfp = Buf("lfp", self.uext.ap[:, 0:npb * NH].rearrange("p (b h) -> p b h", h=NH))
        self.arena_users = kTts + ksts
        self.load_group_tables("p")
        for l in range(DEPTH):
            self.memset("dve", self.carry[l][:, :], 0.0, [self.carry[l]])
        ctmp = self.sb("ctmp", (128, NH))
        for l in range(DEPTH):
            for s in range(SB_):
                self.dma("sync", lfp[:, :, :], I["clf"][l, s].rearrange("(b p) h -> p b h", p=128), [], [lfp])
                self.memset("dve", ctmp[:, :], 0.0, [ctmp])
                for b in range(npb):
                    cps = self.bank()
                    self.mm(cps[:, 0:NH], self.tri[:], lfp[:, b, :], [self.tri, lfp], [cps])
                    self.mm(cps[:, 16:16 + NH], self.ones[:], lfp[:, b, :], [self.ones, lfp], [cps])
                    self.stt("dve", self.nch_s[l][s][:, b, :], cps[:, 0:NH], -1.0, ctmp[:, :], ALU.mult, ALU.subtract, [cps, ctmp], [self.nch_s[l][s]])
                    self.tt("dve", ctmp[:, :], ctmp[:, :], cps[:, 16:16 + NH], ALU.add, [cps, ctmp], [ctmp])
                self.stt("dve", self.carry[l][:, :], ctmp[:, :], self.segmask[:, s:s + 1], self.carry[l][:, :], ALU.mult, ALU.add, [ctmp, self.segmask, self.carry[l]], [self.carry[l]])
                pass
        units = [(l, s, c) for l in range(DEPTH) for s in range(SB_) for c in range(PAST // 512)]
        vbs = {}
        for l in range(DEPTH):
            for s in range(SB_):
                vb_ = Buf("vhist"); vbs[(l, s)] = vb_
                self.dma("pool", self.vs_s[l][s][:, :, :].rearrange("(b p) h d -> p b (h d)", p=128),
                         I["cv"][l, s, :, :].rearrange("(b p) f -> p b f", p=128), [], [vb_])

        def kload(i):
            l, s, c = units[i]
            kst = ksts[i % 2]
            self.dma("pool", kst[:, :].rearrange("p (b f) -> p b f", f=512), I["ck"][l, s, c * 512:(c + 1) * 512, :].rearrange("(b p) f -> p b f", p=128), [], [kst])
        kload(0)
        for i, (l, s, c) in enumerate(units):
            if i + 1 < len(units):
                kload(i + 1)
            kst = ksts[i % 2]; kTt = kTts[i % 2]
            kb = Buf("kvbs"); self.kvb_s[(l, s, c)] = kb
            alias(kb, [vbs[(l, s)]])
            for b in range(4):
                for hg in range(2):
                    tb = self.tbank()
                    for hh in range(4):
                        h = hg * 4 + hh
                        self.tr(tb.bf[0:64, hh * 128:(hh + 1) * 128], kst[:, b * 512 + h * HD:b * 512 + (h + 1) * HD], self.ident_b[:], [kst, self.ident_b], [tb])
                    self.cp("dve" if hg == 0 else "act", kTt[0:64, :].rearrange("p (h t) -> p h t", t=512)[:, hg * 4:(hg + 1) * 4, b * 128:(b + 1) * 128],
                            tb.bf[0:64, 0:512].rearrange("p (h t) -> p h t", t=128), [tb], [kTt])
            self.dma("pool", self.kts_s[l][s][:, :, c * 512:(c + 1) * 512], kTt[0:65, :].rearrange("p (h t) -> p h t", t=512), [kTt], [kb])
        self.w_pump(10 ** 6)
        alias(self.uext, [lfp])
        self.load_group_tables("s")
        for l in range(DEPTH):
            t = self.tmp32(); ps = self.bank()
            self.dma("sync", t[0:120, :], I["st_conv"][l, :, :], [], [t])
            for c in range(4):
                self.tr(ps[:, c * 120:(c + 1) * 120], t[0:120, c * 128:(c + 1) * 128], self.ident[0:120, 0:120], [t, self.ident], [ps])
            self.cp("dve", self.u_ctx[l][:, :, :, :], ps[:, 0:480].rearrange("p (c s w) -> p c s w", c=4, s=SB_), [ps], [self.u_ctx[l]])
            ps2 = self.bank()
            for i6 in range(6):
                nco = 512 if i6 < 5 else 256
                fs = self.tmp32()
                self.dma("sync", fs[0:8, 0:nco], I["st_ffn"][l, :, i6 * 512:i6 * 512 + nco], [], [fs])
                for ii in range(nco // 128):
                    i = i6 * 4 + ii
                    self.tr(ps2[:, i * 8:(i + 1) * 8], fs[0:8, ii * 128:(ii + 1) * 128], self.ident[0:8, 0:8], [fs, self.ident], [ps2])
            self.cp("dve", self.a_ctx[l][:, :, :, :], ps2[:, 0:NFC * 8].rearrange("p (c s w) -> p c s w", c=NFC, s=SB_), [ps2], [self.a_ctx[l]])
            for par in range(2):
                for s in range(SB_):
                    self.dma("sync", self.rstate[l][par * 64:(par + 1) * 64, s, :, :],
                             I["st_ret"][l, s].rearrange("(c two) d v -> two d c v", two=2)[par], [], [self.rstate[l]])
            self.cp("pool", self.rstate_b[l][:, :, :, :], self.rstate[l][:, :, :, :], [self.rstate[l]], [self.rstate_b[l]])
        self.dma("sync", self.x[:, 0, :], I["xs"][:, :], [], [self.xb[0]])
        tinfo = {"last": True, "t0": 0}
        for l in range(DEPTH):
            self.layer(G, l, tinfo)
            for par in range(2):
                for s in range(SB_):
                    self.dma("pool", O["s_ret"][l, s].rearrange("(c two) d v -> two d c v", two=2)[par], self.rstate[l][par * 64:(par + 1) * 64, s, :, :], [self.rstate[l]], [], is_out=True)
        self.dma("pool", O["y_s"][:, :], self.x[:, 0, :], [self.xb[0]], [], is_out=True)

    def prompt_group(self, NT):
        I, O = self.I, self.O
        self.load_group_tables("p")
        G = {"name": "p", "T": 512, "nblk": 4, "nseg": 1, "L": 512}
        for sq in range(self.NSP):
            for l in range(DEPTH):
                self.memset("dve", self.carry[l][:, :], 0.0, [self.carry[l]])
                self.memset("dve", self.u_ctx[l][:, :, :, :], 0.0, [self.u_ctx[l]])
                self.memset("dve", self.a_ctx[l][:, :, :, :], 0.0, [self.a_ctx[l]])
                self.memset("dve", self.rstate[l][:, :, :, :], 0.0, [self.rstate[l]])
                self.memset("pool", self.rstate_b[l][:, :, :, :], 0.0, [self.rstate_b[l]])
            for ti in range(NT):
                t0 = ti * 512
                for b in range(4):
                    self.dma("sync", self.x[:, b, :], I["xp"][sq, t0 + b * 128:t0 + (b + 1) * 128, :], [], [self.xb[b]])
                self.dma("sync", self.cosT[:, :], I["c_cos_p"][:, t0:t0 + 512], [], [self.cosT])
                self.dma("sync", self.sinT[:, :], I["c_sin_p"][:, t0:t0 + 512], [], [self.sinT])
                tinfo = {"last": ti == NT - 1, "t0": t0, "seq": sq}
                for l in range(DEPTH):
                    self.layer(G, l, tinfo)
                    if ti == NT - 1:
                        for par in range(2):
                            self.dma("pool", O["p_ret"][l, sq].rearrange("(c two) d v -> two d c v", two=2)[par], self.rstate[l][par * 64:(par + 1) * 64, 0, :, :], [self.rstate[l]], [], is_out=True)
                for b in range(4):
                    self.dma("pool", O["y_p"][sq, t0 + b * 128:t0 + (b + 1) * 128, :], self.x[:, b, :], [self.xb[b]], [], is_out=True)


_CACHE = {}


def kernel(x_prompt, x_sample, state_conv, state_ret, cache_fox_k, cache_fox_v, cache_fox_logf, state_ffn_conv, **w):
    f = np.float32
    BP, SEQ, _ = x_prompt.shape
    NSP = BP // NCORES
    PAST = cache_fox_k.shape[2]
    key = (NSP, SEQ, PAST)
    if key not in _CACHE:
        _CACHE[key] = Kern(NSP, SEQ, PAST).build()
    nc = _CACHE[key]
    ct = const_tables(SEQ, PAST)
    wts = {n: np.ascontiguousarray(np.asarray(w[n], dtype=f)) for n in W_SHAPES}
    in_maps = []
    for c in range(NCORES):
        ss = slice(c * SB_, (c + 1) * SB_)
        m = {
            "xp": np.ascontiguousarray(x_prompt[c * NSP:(c + 1) * NSP], dtype=f),
            "xs": np.ascontiguousarray(x_sample[ss], dtype=f).reshape(128, D),
            "st_conv": np.ascontiguousarray(state_conv[:, ss], dtype=f).reshape(DEPTH, SB_ * 30, DC),
            "st_ret": np.ascontiguousarray(state_ret[:, ss], dtype=f),
            "ck": np.ascontiguousarray(cache_fox_k[:, ss], dtype=f).reshape(DEPTH, SB_, PAST, 512),
            "cv": np.ascontiguousarray(cache_fox_v[:, ss], dtype=f).reshape(DEPTH, SB_, PAST, 512),
            "clf": np.ascontiguousarray(cache_fox_logf[:, ss], dtype=f),
            "st_ffn": np.ascontiguousarray(state_ffn_conv[:, ss], dtype=f).reshape(DEPTH, SB_ * 2, DFF),
        }
        m.update(wts); m.update(ct)
        in_maps.append(m)
    res = run_bass_kernel_spmd(nc, in_maps, core_ids=list(range(NCORES)))
    R = res.results
    cat = lambda k, ax: np.concatenate([np.asarray(r[k]) for r in R], axis=ax)
    y_p = cat("y_p", 0)
    y_s = cat("y_s", 0).reshape(NCORES * SB_, LS, D)
    p_conv = cat("p_conv", 1); p_ret = cat("p_ret", 1)
    p_k = cat("p_k", 1).reshape(DEPTH, BP, SEQ, NH, HD); p_v = cat("p_v", 1).reshape(DEPTH, BP, SEQ, NH, HD)
    p_lf = cat("p_lf", 1); p_ffn = cat("p_ffn", 1)
    s_conv = cat("s_conv", 1).reshape(DEPTH, NCORES * SB_, 30, DC); s_ret = cat("s_ret", 1)
    s_k = cat("s_k", 1).reshape(DEPTH, NCORES * SB_, LS, NH, HD); s_v = cat("s_v", 1).reshape(DEPTH, NCORES * SB_, LS, NH, HD)
    s_lf = cat("s_lf", 1).reshape(DEPTH, NCORES * SB_, LS, NH); s_ffn = cat("s_ffn", 1).reshape(DEPTH, NCORES * SB_, 2, DFF)
    return (y_p, y_s, p_conv, p_ret, p_k, p_v, p_lf, p_ffn, s_conv, s_ret, s_k, s_v, s_lf, s_ffn)
```
